# Optimizing a Trainium2 kernel written in Bass

```python
import jax, jax.numpy as jnp
from jax import lax
import numpy as np

D_MODEL = 1024
BATCH = 8
SEQ = 2048
DEPTH = 4
DEC_BATCH = 32
DEC_SEQ = 2048
PAST_LEN = 128

D_MIX = D_MODEL
D_CONV = D_MIX // 2
CONV_HEAD = 64
N_CONV_HEADS = D_CONV // CONV_HEAD
D_FOURIER = D_MIX - D_CONV
N_FOURIER_GROUPS = 8
FOURIER_GROUP = D_FOURIER // N_FOURIER_GROUPS
CONV_WIDTH = 31
CONV_PAD = CONV_WIDTH // 2
D_IN = 2 * D_CONV + D_FOURIER
D_FF = 2816
RMS_EPS = 1e-6
LN_EPS = 1e-5

kernel_name = "hybrid_conformer_conv_fnet_encoder"


def rms_norm(x, g):
    xf = x.astype(jnp.float32)
    y = xf * lax.rsqrt(jnp.mean(xf * xf, axis=-1, keepdims=True) + RMS_EPS)
    return (y * g.astype(jnp.float32)).astype(x.dtype)


def layer_norm(x, g, b):
    xf = x.astype(jnp.float32)
    mu = jnp.mean(xf, axis=-1, keepdims=True)
    var = jnp.mean(jnp.square(xf - mu), axis=-1, keepdims=True)
    y = (xf - mu) * lax.rsqrt(var + LN_EPS)
    return (y * g.astype(jnp.float32) + b.astype(jnp.float32)).astype(x.dtype)


def swiglu(h, w_gate, w_up, w_down):
    a = jnp.einsum('bsd,df->bsf', h, w_gate)
    u = jnp.einsum('bsd,df->bsf', h, w_up)
    return jnp.einsum('bsf,fd->bsd', jax.nn.silu(a) * u, w_down)


def depthwise_conv(u, w, b):
    c = u.shape[-1]
    out = lax.conv_general_dilated(
        u, w[:, None, :].astype(u.dtype), window_strides=(1,),
        padding=[(CONV_PAD, CONV_PAD)], dimension_numbers=('NWC', 'WIO', 'NWC'),
        feature_group_count=c)
    return out + b


def fourier_mix(u):
    bsz, s, _ = u.shape
    ug = u.reshape(bsz, s, N_FOURIER_GROUPS, FOURIER_GROUP).astype(jnp.float32)
    f = jnp.fft.fft2(ug, axes=(1, 3), norm='ortho').real
    return f.reshape(bsz, s, D_FOURIER).astype(u.dtype)


def trunk(x, ffn1_norm, ffn1_w_gate, ffn1_w_up, ffn1_w_down, mix_norm, w_in,
          conv_w, conv_b, conv_ln_g, conv_ln_b, w_out, ffn2_norm, ffn2_w_gate,
          ffn2_w_up, ffn2_w_down, final_norm):
    for l in range(DEPTH):
        x = x + 0.5 * swiglu(rms_norm(x, ffn1_norm[l]), ffn1_w_gate[l], ffn1_w_up[l], ffn1_w_down[l])
        h = rms_norm(x, mix_norm[l])
        p = jnp.einsum('bsd,dk->bsk', h, w_in[l])
        c_val = p[..., :D_CONV]
        c_gate = p[..., D_CONV:2 * D_CONV]
        f_in = p[..., 2 * D_CONV:]
        c = c_val * jax.nn.sigmoid(c_gate)
        c = depthwise_conv(c, conv_w[l], conv_b[l])
        c = jax.nn.silu(layer_norm(c, conv_ln_g[l], conv_ln_b[l]))
        f = fourier_mix(f_in)
        m = jnp.concatenate([c, f], axis=-1)
        x = x + jnp.einsum('bsk,kd->bsd', m, w_out[l])
        x = x + 0.5 * swiglu(rms_norm(x, ffn2_norm[l]), ffn2_w_gate[l], ffn2_w_up[l], ffn2_w_down[l])
    return rms_norm(x, final_norm)


def setup_inputs(seed: int = 0) -> dict:
    key = jax.random.key(seed)
    ks = jax.random.split(key, 20)
    f32 = jnp.float32

    def w(k, shape, fan_in):
        return jax.random.normal(k, shape, f32) * (fan_in ** -0.5)

    def gain(k, shape):
        return 1.0 + 0.02 * jax.random.normal(k, shape, f32)

    return {
        "x_prompt": jax.random.normal(ks[0], (BATCH, SEQ, D_MODEL), f32),
        "x_sample": jax.random.normal(ks[1], (DEC_BATCH, DEC_SEQ, D_MODEL), f32),
        "ffn1_norm": gain(ks[2], (DEPTH, D_MODEL)),
        "ffn1_w_gate": w(ks[3], (DEPTH, D_MODEL, D_FF), D_MODEL),
        "ffn1_w_up": w(ks[4], (DEPTH, D_MODEL, D_FF), D_MODEL),
        "ffn1_w_down": w(ks[5], (DEPTH, D_FF, D_MODEL), D_FF),
        "mix_norm": gain(ks[6], (DEPTH, D_MODEL)),
        "w_in": w(ks[7], (DEPTH, D_MODEL, D_IN), D_MODEL),
        "conv_w": w(ks[8], (DEPTH, CONV_WIDTH, D_CONV), CONV_WIDTH),
        "conv_b": 0.02 * jax.random.normal(ks[9], (DEPTH, D_CONV), f32),
        "conv_ln_g": gain(ks[10], (DEPTH, D_CONV)),
        "conv_ln_b": 0.02 * jax.random.normal(ks[11], (DEPTH, D_CONV), f32),
        "w_out": w(ks[12], (DEPTH, D_MIX, D_MODEL), D_MIX),
        "ffn2_norm": gain(ks[13], (DEPTH, D_MODEL)),
        "ffn2_w_gate": w(ks[14], (DEPTH, D_MODEL, D_FF), D_MODEL),
        "ffn2_w_up": w(ks[15], (DEPTH, D_MODEL, D_FF), D_MODEL),
        "ffn2_w_down": w(ks[16], (DEPTH, D_FF, D_MODEL), D_FF),
        "final_norm": gain(ks[17], (D_MODEL,)),
    }


def reference(x_prompt, x_sample, ffn1_norm, ffn1_w_gate, ffn1_w_up, ffn1_w_down,
              mix_norm, w_in, conv_w, conv_b, conv_ln_g, conv_ln_b, w_out,
              ffn2_norm, ffn2_w_gate, ffn2_w_up, ffn2_w_down, final_norm):
    y_prompt = trunk(x_prompt, ffn1_norm, ffn1_w_gate, ffn1_w_up, ffn1_w_down, mix_norm,
                     w_in, conv_w, conv_b, conv_ln_g, conv_ln_b, w_out, ffn2_norm,
                     ffn2_w_gate, ffn2_w_up, ffn2_w_down, final_norm)
    y_sample = trunk(x_sample, ffn1_norm, ffn1_w_gate, ffn1_w_up, ffn1_w_down, mix_norm,
                     w_in, conv_w, conv_b, conv_ln_g, conv_ln_b, w_out, ffn2_norm,
                     ffn2_w_gate, ffn2_w_up, ffn2_w_down, final_norm)
    return (y_prompt, y_sample)
```

```python
import contextlib
import numpy as np
import concourse.bass as bass
import concourse.mybir as mybir
from concourse.bass_utils import run_bass_kernel_spmd

F32 = mybir.dt.float32
BF16 = mybir.dt.bfloat16
ALU = mybir.AluOpType
AF = mybir.ActivationFunctionType

D = 1024
S = 2048
DEPTH = 4
DFF = 2816
NFC = DFF // 128
NDC = D // 128
TT = 512
NT = S // TT
NCORES = 8
NSEQ = 5
CW = 31
RMS_EPS = 1e-6
LN_EPS = 1e-5

SLOT = 4096
NSLOT = 5

GU_SZ = NFC * 2 * NDC * 128
DW_SZ = NDC * NFC * 128
VG_SZ = 4 * 2 * NDC * 128
WF_SZ = NDC * 512
WO_SZ = NDC * NDC * 128
L_SZ = 2 * (GU_SZ + DW_SZ) + VG_SZ + WF_SZ + WO_SZ
OFF_GU1 = 0
OFF_D1 = OFF_GU1 + GU_SZ
OFF_VG = OFF_D1 + DW_SZ
OFF_WF = OFF_VG + VG_SZ
OFF_WO = OFF_WF + WF_SZ
OFF_GU2 = OFF_WO + WO_SZ
OFF_D2 = OFF_GU2 + GU_SZ
OFF_DFT = DEPTH * L_SZ
DFT_SZ = 4 * 16 * 2 * 512
OFF_CCS = OFF_DFT + DFT_SZ
CCS_SZ = 256
WTOT = OFF_CCS + CCS_SZ
PCH = 4096
NPCH = (WTOT + PCH - 1) // PCH
WTOTP = NPCH * PCH

SM_L = 24 + 4 * CW + 12
SM_G1, SM_GM, SM_G2, SM_CW, SM_CB, SM_LG, SM_LB = 0, 8, 16, 24, 24 + 124, 24 + 128, 24 + 132
SM_FIN = DEPTH * SM_L
SM_ID = SM_FIN + 8
SM_TOT = SM_ID + 128
NPE = 12
OFF_DIAG = WTOTP
DIAG_H = 2 * NPE * 128
DIAG_SZ = 2 * DIAG_H


class Buf:
    __slots__ = ("name", "w", "r", "ov")

    def __init__(self, name):
        self.name = name
        self.w = None
        self.r = {}
        self.ov = []


def overlap(a_list, b_list):
    for a in a_list:
        for b in b_list:
            a.ov.append(b)
            b.ov.append(a)


class Sched:
    def __init__(self, nc, stack):
        self.nc = nc
        self.stack = stack
        self.eng = {"pe": nc.tensor, "act": nc.scalar, "dve": nc.vector, "pool": nc.gpsimd, "sp": nc.sync}
        self.sems = {}
        self.cnt = {}
        self.known = {e: {} for e in self.eng}
        for e in ("pe", "act", "dve", "pool"):
            self.sem(e)

    def sem(self, key):
        if key not in self.sems:
            name = "s_" + "_".join(str(k) for k in (key if isinstance(key, tuple) else (key,)))
            self.sems[key] = self.stack.enter_context(self.nc.semaphore(name))
            self.cnt[key] = 0
        return self.sems[key]

    def need(self, e, tk, war=False):
        if tk is None:
            return
        key, val = tk
        if key == e:
            if e == "pe" or war:
                return
        if self.known[e].get(key, 0) >= val:
            return
        self.eng[e].wait_ge(self.sems[key], val)
        self.known[e][key] = val

    def deps(self, e, reads, writes):
        for b in reads:
            self.need(e, b.w)
        for b in writes:
            for bb in [b] + b.ov:
                self.need(e, bb.w)
                for k, v in bb.r.items():
                    self.need(e, (k, v), war=True)

    def commit(self, tk, reads, writes):
        for b in reads:
            b.r[tk[0]] = tk[1]
        for b in writes:
            b.w = tk
            b.r = {}

    def op(self, e, fn, reads=(), writes=()):
        self.deps(e, reads, writes)
        ins = fn(self.eng[e])
        self.cnt[e] += 1
        ins.then_inc(self.sems[e], 1)
        self.commit((e, self.cnt[e]), reads, writes)

    def mm(self, out_buf, out_ap, items):
        tk = ("pe", self.cnt["pe"] + 1)
        self.deps("pe", (), (out_buf,))
        n = len(items)
        allreads = []
        for i, (l, r, rd) in enumerate(items):
            self.deps("pe", rd, ())
            ins = self.nc.tensor.matmul(out_ap, l, r, start=(i == 0), stop=(i == n - 1))
            allreads.extend(rd)
        ins.then_inc(self.sems["pe"], 1)
        self.cnt["pe"] += 1
        self.commit(tk, allreads, (out_buf,))

    def dma(self, semkey, out_ap, in_ap, reads=(), writes=(), e="sp"):
        self.sem(semkey)
        self.deps(e, reads, writes)
        self.eng[e].dma_start(out=out_ap, in_=in_ap).then_inc(self.sems[semkey], 16)
        self.cnt[semkey] += 16
        self.commit((semkey, self.cnt[semkey]), reads, writes)


def build(nseq=NSEQ, depth=DEPTH):
    nc = bass.Bass("TRN2", target_bir_lowering=False)
    xT = nc.dram_tensor("xT", [nseq, D, S], F32, kind="ExternalInput").ap()
    wflat = nc.dram_tensor("wflat", [128, WTOTP], F32, kind="ExternalInput").ap()
    small = nc.dram_tensor("small", [128, SM_TOT], F32, kind="ExternalInput").ap()
    yT = nc.dram_tensor("yT", [nseq, D, S], F32, kind="ExternalOutput").ap()
    wbf = nc.dram_tensor("wbf", [128, WTOTP + DEPTH * DIAG_SZ], BF16, kind="Internal").ap()

    with contextlib.ExitStack() as stack:
        sc = Sched(nc, stack)
        SM = stack.enter_context(nc.sbuf_tensor("SM", [128, SM_TOT], F32))
        ONES = stack.enter_context(nc.sbuf_tensor("ONES", [128, 256], BF16))
        CCS = stack.enter_context(nc.sbuf_tensor("CCS", [128, 256], BF16))
        EPS = stack.enter_context(nc.sbuf_tensor("EPS", [128, 2], F32))
        bSM, bONES, bCCS, bNEGH = Buf("SM"), Buf("ONES"), Buf("CCS"), Buf("NEGH")

        LCH = L_SZ // PCH
        GCH = GU_SZ // PCH
        groups = [("w0a", range(0, GCH)), ("w0", range(GCH, LCH)), ("wc", range(DEPTH * LCH, NPCH))] + \
                 [(f"w{l}", range(l * LCH, (l + 1) * LCH)) for l in range(1, DEPTH)]
        bW = {}
        for gname, chunks in groups:
            bW[gname] = Buf(gname)
            for i in chunks:
                sc.sem(("cv", gname))
                nc.gpsimd.dma_start(out=wbf[:, i * PCH:(i + 1) * PCH], in_=wflat[:, i * PCH:(i + 1) * PCH],
                                    max_dma_last_dim=8192).then_inc(sc.sems[("cv", gname)], 16)
                sc.cnt[("cv", gname)] += 16
            bW[gname].w = (("cv", gname), sc.cnt[("cv", gname)])

        def wgroup(off):
            if off >= OFF_DIAG:
                return bW["dg"]
            if off >= OFF_DFT:
                return bW["wc"]
            if off < GU_SZ:
                return bW["w0a"]
            return bW[f"w{off // L_SZ}"]

        sc.dma("sm", SM[:], small[:, :], writes=(bSM,))
        sc.dma("ccs", CCS[:], wbf[:, OFF_CCS:OFF_CCS + CCS_SZ], reads=(bW["wc"],), writes=(bCCS,))

        A_X = 0
        A_HT = A_X + NDC * S * 2
        A_RING = A_HT + 2 * NDC * TT
        A_SIL = A_RING + NSLOT * SLOT
        A_RSTD = A_SIL + 3 * TT * 2
        A_C = A_RSTD + 2 * TT * 2
        CWID = S + 32
        A_MT = A_C + 4 * CWID
        A_PQ = A_MT + 2 * NDC * TT
        A_SQ = A_PQ + 4 * TT
        A_U = A_SQ + NDC * TT
        A_LN = A_U + 16 * 512
        A_CA1 = A_LN + 4 * TT * 2
        A_END_M = A_CA1 + 2 * TT * 2
        A_GT = A_U
        A_END = max(A_END_M, A_GT + NFC * TT)
        AR = stack.enter_context(nc.sbuf_tensor("ARENA", [128, A_END], BF16))

        def f32v(off, n):
            return AR[:, off:off + 2 * n].bitcast(F32)

        X = f32v(A_X, NDC * S).rearrange("p (c t) -> p c t", c=NDC)
        HT = [AR[:, A_HT + s * NDC * TT: A_HT + (s + 1) * NDC * TT].rearrange("p (c t) -> p c t", c=NDC) for s in range(2)]
        RING = [AR[:, A_RING + s * SLOT: A_RING + (s + 1) * SLOT] for s in range(NSLOT)]
        SIL = [f32v(A_SIL + s * TT * 2, TT) for s in range(3)]
        LNS = [AR[:, A_SIL + j * TT: A_SIL + (j + 1) * TT] for j in range(4)]
        RSTD = [f32v(A_RSTD + s * TT * 2, TT) for s in range(2)]
        C = AR[:, A_C:A_C + 4 * CWID].rearrange("p (j t) -> p j t", j=4)
        MT = [AR[:, A_MT + s * NDC * TT: A_MT + (s + 1) * NDC * TT].rearrange("p (c t) -> p c t", c=NDC) for s in range(2)]
        PQ = [AR[:, A_PQ + s * TT: A_PQ + (s + 1) * TT] for s in range(4)]
        SQ = AR[:, A_SQ:A_SQ + NDC * TT].rearrange("p (c t) -> p c t", c=NDC)
        CACC0 = [f32v(A_SQ + s * TT * 2, TT) for s in range(4)]
        U = AR[:, A_U:A_U + 16 * 512].rearrange("p (k c) -> p k c", k=16)
        LN = [f32v(A_LN + s * TT * 2, TT) for s in range(4)]
        CACC1 = [LN[2], LN[3], f32v(A_CA1, TT), f32v(A_CA1 + TT * 2, TT)]
        CACCS = [CACC0, CACC1]
        GT = AR[:, A_GT:A_GT + NFC * TT].rearrange("p (f t) -> p f t", f=NFC)

        bX = [[Buf(f"X{c}_{t}") for t in range(NT)] for c in range(NDC)]
        bHT = [[Buf(f"HT{s}_{c}") for c in range(NDC)] for s in range(2)]
        bRING = [Buf(f"RING{s}") for s in range(NSLOT)]
        bSIL = [Buf(f"SIL{s}") for s in range(3)]
        bLNS = [Buf(f"LNS{j}") for j in range(4)]
        for j in range(4):
            overlap([bLNS[j]], [bSIL[j // 2]])
        bRSTD = [Buf(f"RSTD{s}") for s in range(2)]
        bC = [[Buf(f"C{j}_{t}") for t in range(NT)] for j in range(4)]
        bMT = [[Buf(f"MT{s}_{c}") for c in range(NDC)] for s in range(2)]
        bPQ = [Buf(f"PQ{s}") for s in range(4)]
        bSQ = [Buf(f"SQ{c}") for c in range(NDC)]
        bCACC0 = [Buf(f"CACC{s}") for s in range(4)]
        bU = [Buf(f"U{k}") for k in range(16)]
        bLN = [Buf(f"LN{s}") for s in range(4)]
        bCACCS = [bCACC0, [bLN[2], bLN[3], Buf("CA1_2"), Buf("CA1_3")]]
        bGT = [Buf(f"GT{f}") for f in range(NFC)]
        for s in range(4):
            overlap([bCACC0[s]], [bSQ[2 * s], bSQ[2 * s + 1]])
        for f in range(NFC):
            lo, hi = f * TT, (f + 1) * TT
            for k in range(16):
                if lo < (k + 1) * 512 and k * 512 < hi:
                    overlap([bGT[f]], [bU[k]])
            for s in range(4):
                a0 = 16 * 512 + s * TT * 2
                if lo < a0 + TT * 2 and a0 < hi:
                    overlap([bGT[f]], [bLN[s]])

        PS = [stack.enter_context(nc.psum_tensor(f"PS{i}", [128, 512], F32)) for i in range(8)]
        bPS = [Buf(f"PS{i}") for i in range(8)]
        pools = {"A": [0, 1, 7], "U": [2, 3, 5], "Y": [4, 5], "S": [6], "T": [7, 4]}
        pcnt = {k: 0 for k in pools}

        def ps(pool):
            i = pools[pool][pcnt[pool] % len(pools[pool])]
            pcnt[pool] += 1
            return bPS[i], PS[i]

        sc.op("dve", lambda e: e.memset(ONES[:, 0:128], 1.0 / D), writes=(bONES,))
        sc.op("dve", lambda e: e.memset(ONES[:, 128:256], 1.0 / 512), writes=(bONES,))
        sc.op("dve", lambda e: e.memset(EPS[:, 0:1], RMS_EPS), writes=(bNEGH,))
        sc.op("dve", lambda e: e.memset(EPS[:, 1:2], LN_EPS), writes=(bNEGH,))
        sc.op("dve", lambda e: e.memset(C[:, :, 0:16], 0.0), writes=[bC[j][0] for j in range(4)])
        sc.op("dve", lambda e: e.memset(C[:, :, 15 + S:CWID], 0.0), writes=[bC[j][NT - 1] for j in range(4)])
        for l in range(depth):
            o = l * SM_L + SM_CW
            sc.op("dve", lambda e, o=o: e.tensor_scalar(SM[:, o:o + 4 * CW], SM[:, o:o + 4 * CW], 0.5, None, ALU.mult),
                  reads=(bSM,), writes=(bSM,))

        bW["dg"] = Buf("dg")
        for l in range(depth):
            o = l * SM_L + SM_CW
            for jp in range(2):
                sl = (2 * l + jp) % NSLOT
                for jj in range(2):
                    j = jp * 2 + jj
                    for k in range(NPE):
                        sc.op("dve", lambda e, j=j, jj=jj, k=k: e.tensor_scalar(
                            RING[sl][:, (jj * NPE + k) * 128:(jj * NPE + k + 1) * 128], SM[:, SM_ID:SM_ID + 128],
                            SM[:, o + j * CW + k:o + j * CW + k + 1], None, ALU.mult),
                            reads=(bSM,), writes=(bRING[sl],))
                off = OFF_DIAG + l * DIAG_SZ + jp * DIAG_H
                sc.dma("dg", wbf[:, off:off + DIAG_H], RING[sl][:, 0:DIAG_H], reads=(bRING[sl],), writes=(bW["dg"],))

        ring_i = [0]

        def ring_load(off, n):
            s = ring_i[0] % NSLOT
            ring_i[0] += 1
            sc.dma(("ring", s), RING[s][:, 0:n], wbf[:, off:off + n], reads=(wgroup(off),), writes=(bRING[s],))
            return s

        sil_i = [0]
        rstd_i = [0]

        def norm_sq(tt):
            tsl = slice(tt * TT, (tt + 1) * TT)
            for c in range(NDC):
                sc.op("act", lambda e, c=c: e.activation(SQ[:, c, :], X[:, c, tsl], AF.Square),
                      reads=(bX[c][tt],), writes=(bSQ[c],))
            for a, b in ((0, 1), (2, 3), (4, 5), (6, 7), (0, 2), (4, 6), (0, 4)):
                sc.op("dve", lambda e, a=a, b=b: e.tensor_tensor(SQ[:, a, :], SQ[:, a, :], SQ[:, b, :], ALU.add),
                      reads=(bSQ[a], bSQ[b]), writes=(bSQ[a],))

        def norm_rstd():
            bS, pS = ps("S")
            sc.mm(bS, pS[:], [(ONES[:, 0:128], SQ[:, 0, :], (bSQ[0], bONES))])
            r = rstd_i[0] % 2
            rstd_i[0] += 1
            sc.op("act", lambda e: e.activation(RSTD[r][:], pS[:], AF.Ln, bias=EPS[:, 0:1]),
                  reads=(bS, bNEGH), writes=(bRSTD[r],))
            sc.op("act", lambda e: e.activation(RSTD[r][:], RSTD[r][:], AF.Exp, scale=-0.5),
                  reads=(bRSTD[r],), writes=(bRSTD[r],))
            return r

        def norm_stats(tt):
            norm_sq(tt)
            return norm_rstd()

        def norm_h_fin(tt, goff, hs):
            r = norm_rstd()
            tsl = slice(tt * TT, (tt + 1) * TT)
            for c in range(NDC):
                sc.op("dve", lambda e, c=c: e.scalar_tensor_tensor(
                    HT[hs][:, c, :], X[:, c, tsl], SM[:, goff + c:goff + c + 1], RSTD[r][:], ALU.mult, ALU.mult),
                    reads=(bX[c][tt], bRSTD[r], bSM), writes=(bHT[hs][c],))

        def norm_h(tt, goff, hs):
            norm_sq(tt)
            norm_h_fin(tt, goff, hs)

        def ffn_phase1(l, which, tt, hs):
            base = l * L_SZ + (OFF_GU1 if which == 0 else OFF_GU2)
            slot = None
            for fc in range(NFC):
                if fc % 2 == 0:
                    slot = ring_load(base + fc * 2048, 4096)
                o = (fc % 2) * 2048
                bA, pA = ps("A")
                bU_, pU = ps("U")
                sc.mm(bA, pA[:], [(RING[slot][:, o + c * 128:o + (c + 1) * 128], HT[hs][:, c, :],
                                   (bRING[slot], bHT[hs][c])) for c in range(NDC)])
                sc.mm(bU_, pU[:], [(RING[slot][:, o + 1024 + c * 128:o + 1024 + (c + 1) * 128], HT[hs][:, c, :],
                                    (bRING[slot], bHT[hs][c])) for c in range(NDC)])
                si = sil_i[0] % 3
                sil_i[0] += 1
                sc.op("act", lambda e: e.activation(SIL[si][:], pA[:], AF.Silu), reads=(bA,), writes=(bSIL[si],))
                sc.op("dve", lambda e: e.tensor_tensor(GT[:, fc, :], pU[:], SIL[si][:], ALU.mult),
                      reads=(bU_, bSIL[si]), writes=(bGT[fc],))

        def ffn_phase2(l, which, tt, mid=None):
            base = l * L_SZ + (OFF_D1 if which == 0 else OFF_D2)
            tsl = slice(tt * TT, (tt + 1) * TT)
            for dco in range(NDC):
                if dco == 5 and mid is not None:
                    mid()
                slot = ring_load(base + dco * NFC * 128, NFC * 128)
                bY, pY = ps("Y")
                sc.mm(bY, pY[:], [(RING[slot][:, fc * 128:(fc + 1) * 128], GT[:, fc, :], (bRING[slot], bGT[fc]))
                                  for fc in range(NFC)])
                sc.op("dve", lambda e: e.scalar_tensor_tensor(X[:, dco, tsl], pY[:], 0.5, X[:, dco, tsl], ALU.mult, ALU.add),
                      reads=(bY, bX[dco][tt]), writes=(bX[dco][tt],))

        def mixer_in(l, tt, hs, npump=0, mid=None):
            base = l * L_SZ
            col0 = 15 + tt * TT
            for jp in range(2):
                slot = ring_load(base + OFF_VG + jp * 4096, 4096)
                for jj in range(2):
                    j = jp * 2 + jj
                    o = jj * 2048
                    bA, pA = ps("A")
                    bG, pG = ps("U")
                    sc.mm(bA, pA[:], [(RING[slot][:, o + c * 128:o + (c + 1) * 128], HT[hs][:, c, :],
                                       (bRING[slot], bHT[hs][c])) for c in range(NDC)])
                    sc.mm(bG, pG[:], [(RING[slot][:, o + 1024 + c * 128:o + 1024 + (c + 1) * 128], HT[hs][:, c, :],
                                       (bRING[slot], bHT[hs][c])) for c in range(NDC)])
                    si = sil_i[0] % 3
                    sil_i[0] += 1
                    sc.op("act", lambda e: e.activation(SIL[si][:], pG[:], AF.Tanh, scale=0.5), reads=(bG,), writes=(bSIL[si],))
                    sc.op("dve", lambda e: e.scalar_tensor_tensor(C[:, j, col0:col0 + TT], SIL[si][:], 1.0, pA[:], ALU.add, ALU.mult),
                          reads=(bA, bSIL[si]), writes=(bC[j][tt],))
                    pump(npump)
            if mid is not None:
                mid()
            slot = ring_load(base + OFF_WF, 4096)
            for sub in range(4):
                k = tt * 4 + sub
                bT, pT = ps("T")
                sc.mm(bT, pT[:], [(HT[hs][:, c, sub * 128:(sub + 1) * 128], RING[slot][:, c * 512:(c + 1) * 512],
                                   (bRING[slot], bHT[hs][c])) for c in range(NDC)])
                sc.op("act", lambda e: e.copy(U[:, k, :], pT[:]), reads=(bT,), writes=(bU[k],))
                pump(npump // 3)

        bgq = []

        def pump(n):
            k = 0
            while bgq and k < n:
                bgq.pop(0)[1]()
                k += 1

        def flush_through(tag):
            while bgq and bgq[0][0] <= tag:
                bgq.pop(0)[1]()

        def conv_enqueue(l, tt, tag):
            o = l * SM_L
            CACC, bCACC = CACCS[tt % 2], bCACCS[tt % 2]
            for j in range(4):
                if j % 2 == 0:
                    slot = ring_load(OFF_DIAG + l * DIAG_SZ + (j // 2) * DIAG_H, DIAG_H)
                jj = j % 2
                rd = [bRING[slot], bC[j][tt]]
                if tt > 0:
                    rd.append(bC[j][tt - 1])
                if tt < NT - 1:
                    rd.append(bC[j][tt + 1])
                bP, pP = ps("A") if j % 2 == 0 else ps("U")
                sc.mm(bP, pP[:], [(RING[slot][:, (jj * NPE + k) * 128:(jj * NPE + k + 1) * 128],
                                   C[:, j, tt * TT + k: tt * TT + k + TT], rd) for k in range(NPE)])
                b_ap = SM[:, o + SM_CB + j:o + SM_CB + j + 1]
                sc.op("dve", lambda e: e.tensor_scalar(CACC[j][:], pP[:], b_ap, None, ALU.add),
                      reads=(bP, bSM), writes=(bCACC[j],))
            for kk in range(NPE, CW):
                for j in range(4):
                    def emit(kk=kk, j=j):
                        w_ap = SM[:, o + SM_CW + j * CW + kk:o + SM_CW + j * CW + kk + 1]
                        src = C[:, j, tt * TT + kk: tt * TT + kk + TT]
                        rd = [bC[j][tt], bSM, bCACC[j]]
                        if tt > 0:
                            rd.append(bC[j][tt - 1])
                        if tt < NT - 1:
                            rd.append(bC[j][tt + 1])
                        sc.op("dve", lambda e: e.scalar_tensor_tensor(CACC[j][:], src, w_ap, CACC[j][:], ALU.mult, ALU.add),
                              reads=rd, writes=(bCACC[j],))
                    bgq.append((tag, emit))

        def ln_tile(l, tt, ms):
            o = l * SM_L
            CACC, bCACC = CACCS[tt % 2], bCACCS[tt % 2]
            bS1, pS1 = ps("S")
            items1 = []
            for j in range(4):
                sc.op("act", lambda e, j=j: e.copy(LNS[j][:], CACC[j][:]), reads=(bCACC[j],), writes=(bLNS[j],))
                items1.append((ONES[:, 128:256], LNS[j][:], (bLNS[j], bONES)))
            sc.mm(bS1, pS1[:], items1)
            bS2, pS2 = ps("T")
            items2 = []
            for j in range(4):
                sc.op("act", lambda e, j=j: e.activation(LNS[j][:], CACC[j][:], AF.Square), reads=(bCACC[j],), writes=(bLNS[j],))
                items2.append((ONES[:, 128:256], LNS[j][:], (bLNS[j], bONES)))
            sc.mm(bS2, pS2[:], items2)
            sc.op("act", lambda e: e.copy(LN[0][:], pS1[:]), reads=(bS1,), writes=(bLN[0],))
            sc.op("dve", lambda e: e.tensor_tensor(LN[1][:], LN[0][:], LN[0][:], ALU.mult), reads=(bLN[0],), writes=(bLN[1],))
            sc.op("dve", lambda e: e.tensor_tensor(LN[1][:], pS2[:], LN[1][:], ALU.subtract), reads=(bS2, bLN[1]), writes=(bLN[1],))
            sc.op("act", lambda e: e.activation(LN[1][:], LN[1][:], AF.Ln, bias=EPS[:, 1:2]), reads=(bLN[1], bNEGH), writes=(bLN[1],))
            sc.op("act", lambda e: e.activation(LN[1][:], LN[1][:], AF.Exp, scale=-0.5), reads=(bLN[1],), writes=(bLN[1],))
            for j in range(4):
                sc.op("dve", lambda e, j=j: e.tensor_tensor(CACC[j][:], CACC[j][:], LN[0][:], ALU.subtract),
                      reads=(bCACC[j], bLN[0]), writes=(bCACC[j],))
            for j in range(4):
                sc.op("dve", lambda e, j=j: e.tensor_tensor(CACC[j][:], CACC[j][:], LN[1][:], ALU.mult),
                      reads=(bCACC[j], bLN[1]), writes=(bCACC[j],))
            for j in range(4):
                sc.op("act", lambda e, j=j: e.activation(MT[ms][:, j, :], CACC[j][:], AF.Silu,
                                                         bias=SM[:, o + SM_LB + j:o + SM_LB + j + 1],
                                                         scale=SM[:, o + SM_LG + j:o + SM_LG + j + 1]),
                      reads=(bCACC[j], bSM), writes=(bMT[ms][j],))

        def dft_acc(sb, pair):
            if True:
                acc = [ps("A"), ps("A"), ps("U"), ps("U")]
                items = [[], [], [], []]
                slots = []
                for kg in range(4):
                    slot = ring_load(OFF_DFT + (sb * 4 + kg) * 4096, 4096)
                    slots.append(slot)
                    for kq in range(4):
                        k = kg * 4 + kq
                        for cc in range(2):
                            cj = pair * 2 + cc
                            for q in range(2):
                                bAcc, pAcc = acc[q * 2 + cc]
                                first = (kg == 0 and kq == 0)
                                last = (kg == 3 and kq == 3)
                                if first:
                                    sc.deps("pe", (), (bAcc,))
                                sc.deps("pe", (bRING[slot], bU[k]), ())
                                ins = nc.tensor.matmul(pAcc[:], U[:, k, cj * 128:(cj + 1) * 128],
                                                       RING[slot][:, kq * 1024 + q * 512: kq * 1024 + (q + 1) * 512],
                                                       start=first, stop=last)
                                if last:
                                    ins.then_inc(sc.sems["pe"], 1)
                                    sc.cnt["pe"] += 1
                                    tk = ("pe", sc.cnt["pe"])
                                    sc.commit(tk, [bRING[s_] for s_ in slots] + bU, (bAcc,))
                                elif kq == 3 and cc == 1 and q == 1:
                                    ins.then_inc(sc.sems["pe"], 1)
                                    sc.cnt["pe"] += 1
                                    sc.commit(("pe", sc.cnt["pe"]), [bRING[slot]], ())
                for i in (0, 2, 1, 3):
                    bAcc, pAcc = acc[i]
                    sc.op("act", lambda e, i=i, pAcc=pAcc: e.copy(PQ[i][:], pAcc[:]), reads=(bAcc,), writes=(bPQ[i],))
            return acc

        def dft_fin(sb, ms, pair, acc):
            if True:
                for cc in range(2):
                    cj = pair * 2 + cc
                    bY, pY = ps("Y")
                    sc.mm(bY, pY[:], [(CCS[:, 0:128], PQ[cc][:], (bCCS, bPQ[cc])),
                                      (CCS[:, 128:256], PQ[2 + cc][:], (bCCS, bPQ[2 + cc]))])
                    sc.op("act", lambda e, cj=cj, pY=pY: e.copy(MT[ms][:, 4 + cj, :], pY[:]), reads=(bY,), writes=(bMT[ms][4 + cj],))

        def wout_tile(l, tt, ms, npump=0):
            base = l * L_SZ + OFF_WO
            tsl = slice(tt * TT, (tt + 1) * TT)
            slot = None
            for dco in range(NDC):
                if dco % 4 == 0:
                    slot = ring_load(base + (dco // 4) * 4096, 4096)
                o = (dco % 4) * 1024
                bY, pY = ps("Y")
                sc.mm(bY, pY[:], [(RING[slot][:, o + kc * 128:o + (kc + 1) * 128], MT[ms][:, kc, :], (bRING[slot], bMT[ms][kc]))
                                  for kc in range(NDC)])
                sc.op("dve", lambda e: e.tensor_tensor(X[:, dco, tsl], pY[:], X[:, dco, tsl], ALU.add),
                      reads=(bY, bX[dco][tt]), writes=(bX[dco][tt],))
                pump(npump)

        def final_tile(i, tt):
            r = norm_stats(tt)
            tsl = slice(tt * TT, (tt + 1) * TT)
            for c in range(NDC):
                sc.op("dve", lambda e, c=c: e.scalar_tensor_tensor(
                    X[:, c, tsl], X[:, c, tsl], SM[:, SM_FIN + c:SM_FIN + c + 1], RSTD[r][:], ALU.mult, ALU.mult),
                    reads=(bX[c][tt], bRSTD[r], bSM), writes=(bX[c][tt],))
            sc.dma(("xst", tt), yT[i].rearrange("(c p) t -> p c t", p=128)[:, :, tsl], X[:, :, tsl],
                   reads=[bX[c][tt] for c in range(NDC)], e="pool")

        def load_x(i, tt, e="sp"):
            tsl = slice(tt * TT, (tt + 1) * TT)
            sc.dma(("xld", tt), X[:, :, tsl], xT[i].rearrange("(c p) t -> p c t", p=128)[:, :, tsl],
                   writes=[bX[c][tt] for c in range(NDC)], e=e)

        hs_i = [0]

        def next_hs():
            h = hs_i[0] % 2
            hs_i[0] += 1
            return h

        for i in range(nseq):
            if i == 0:
                for tt in range(NT):
                    load_x(i, tt)
                hs = next_hs()
                norm_h(0, 0 * SM_L + SM_G1, hs)
            for l in range(depth):
                o = l * SM_L
                for tt in range(NT):
                    ffn_phase1(l, 0, tt, hs)
                    hs_next = next_hs()
                    nt_, ng_ = (tt + 1, o + SM_G1) if tt + 1 < NT else (0, o + SM_GM)
                    norm_sq(nt_)
                    ffn_phase2(l, 0, tt, mid=lambda: norm_h_fin(nt_, ng_, hs_next))
                    hs = hs_next
                hsl = [hs, None, None, None]
                hsl[1] = next_hs()
                norm_sq(1)
                mixer_in(l, 0, hsl[0], mid=lambda: norm_h_fin(1, o + SM_GM, hsl[1]))
                hsl[2] = next_hs()
                norm_sq(2)
                mixer_in(l, 1, hsl[1], mid=lambda: norm_h_fin(2, o + SM_GM, hsl[2]))
                hsl[3] = next_hs()
                norm_h(3, o + SM_GM, hsl[3])
                conv_enqueue(l, 0, 0)
                mixer_in(l, 2, hsl[2], npump=6)
                conv_enqueue(l, 1, 1)
                mixer_in(l, 3, hsl[3], npump=6)
                for tt in range(NT):
                    ms = tt % 2
                    acc0 = dft_acc(tt, 0)
                    flush_through(tt)
                    ln_tile(l, tt, ms)
                    if tt + 2 < NT:
                        conv_enqueue(l, tt + 2, tt + 2)
                    dft_fin(tt, ms, 0, acc0)
                    pump(27)
                    acc1 = dft_acc(tt, 1)
                    dft_fin(tt, ms, 1, acc1)
                    wout_tile(l, tt, ms, npump=2)
                    pump(33)
                assert not bgq
                hs = next_hs()
                norm_h(0, o + SM_G2, hs)
                for tt in range(NT):
                    ffn_phase1(l, 1, tt, hs)
                    if l == depth - 1 and tt >= 1:
                        final_tile(i, tt - 1)
                        if i + 1 < nseq:
                            load_x(i + 1, tt - 1, "pool")
                    hs_next = next_hs()
                    nxt = None
                    if tt + 1 < NT:
                        nxt = (tt + 1, o + SM_G2)
                    elif l + 1 < depth:
                        nxt = (0, (l + 1) * SM_L + SM_G1)
                    elif i + 1 < nseq:
                        nxt = (0, 0 * SM_L + SM_G1)
                    if nxt is not None:
                        norm_sq(nxt[0])
                        ffn_phase2(l, 1, tt, mid=lambda: norm_h_fin(nxt[0], nxt[1], hs_next))
                    else:
                        ffn_phase2(l, 1, tt)
                    hs = hs_next
            final_tile(i, NT - 1)
            if i + 1 < nseq:
                load_x(i + 1, NT - 1, "pool")
        for tt in range(NT):
            sc.need("sp", (("xst", tt), sc.cnt[("xst", tt)]))
    return nc


def _host_layout(inp):
    W = np.zeros((128, WTOTP), np.float32)

    def put(off, arr):
        a = np.ascontiguousarray(arr, dtype=np.float32).reshape(128, -1)
        W[:, off:off + a.shape[1]] = a

    for l in range(DEPTH):
        b = l * L_SZ
        for which, (kg, ku, kd, og, od) in enumerate((("ffn1_w_gate", "ffn1_w_up", "ffn1_w_down", OFF_GU1, OFF_D1),
                                                      ("ffn2_w_gate", "ffn2_w_up", "ffn2_w_down", OFF_GU2, OFF_D2))):
            g = inp[kg][l].reshape(NDC, 128, NFC, 128).transpose(1, 2, 0, 3)
            u = inp[ku][l].reshape(NDC, 128, NFC, 128).transpose(1, 2, 0, 3)
            put(b + og, np.stack([g, u], axis=2))
            d = inp[kd][l].reshape(NFC, 128, NDC, 128).transpose(1, 2, 0, 3)
            put(b + od, d)
        wi = inp["w_in"][l]
        v = wi[:, 0:512].reshape(NDC, 128, 4, 128).transpose(1, 2, 0, 3)
        gt = wi[:, 512:1024].reshape(NDC, 128, 4, 128).transpose(1, 2, 0, 3)
        put(b + OFF_VG, np.stack([v, gt], axis=2))
        f = wi[:, 1024:1536].reshape(NDC, 128, 512).transpose(1, 0, 2)
        put(b + OFF_WF, f)
        wo = inp["w_out"][l].reshape(NDC, 128, NDC, 128).transpose(1, 2, 0, 3)
        put(b + OFF_WO, wo)
    scale = 1.0 / np.sqrt(float(S * 64))
    s_idx = np.arange(S, dtype=np.int64)
    m = (s_idx[:, None] * s_idx[None, :]) % S
    ang = 2.0 * np.pi * m.astype(np.float64) / S
    cs = np.stack([np.cos(ang) * scale, -np.sin(ang) * scale], axis=0).astype(np.float32)
    t = cs.reshape(2, 16, 128, 4, 512).transpose(2, 3, 1, 0, 4)
    put(OFF_DFT, t)
    c_idx = np.arange(128)
    same = (c_idx[:, None] // 64) == (c_idx[None, :] // 64)
    a2 = 2.0 * np.pi * ((c_idx[:, None] % 64) * (c_idx[None, :] % 64) % 64) / 64.0
    cc = np.where(same, np.cos(a2), 0.0)
    sn = np.where(same, np.sin(a2), 0.0)
    put(OFF_CCS, np.concatenate([cc, sn], axis=1))

    SMh = np.zeros((128, SM_TOT), np.float32)
    for l in range(DEPTH):
        o = l * SM_L
        SMh[:, o + SM_G1:o + SM_G1 + 8] = inp["ffn1_norm"][l].reshape(8, 128).T
        SMh[:, o + SM_GM:o + SM_GM + 8] = inp["mix_norm"][l].reshape(8, 128).T
        SMh[:, o + SM_G2:o + SM_G2 + 8] = inp["ffn2_norm"][l].reshape(8, 128).T
        cw = inp["conv_w"][l].reshape(CW, 4, 128).transpose(2, 1, 0)
        SMh[:, o + SM_CW:o + SM_CW + 4 * CW] = cw.reshape(128, 4 * CW)
        SMh[:, o + SM_CB:o + SM_CB + 4] = inp["conv_b"][l].reshape(4, 128).T
        SMh[:, o + SM_LG:o + SM_LG + 4] = inp["conv_ln_g"][l].reshape(4, 128).T
        SMh[:, o + SM_LB:o + SM_LB + 4] = inp["conv_ln_b"][l].reshape(4, 128).T
    SMh[:, SM_FIN:SM_FIN + 8] = inp["final_norm"].reshape(8, 128).T
    SMh[:, SM_ID:SM_ID + 128] = np.eye(128, dtype=np.float32)
    return W, SMh


def kernel(**inputs):
    inp = {k: np.asarray(v) for k, v in inputs.items()}
    xp, xs = inp["x_prompt"], inp["x_sample"]
    xall = np.concatenate([xp, xs], axis=0)
    xallT = np.ascontiguousarray(xall.transpose(0, 2, 1))
    W, SMh = _host_layout(inp)
    nc = build()
    in_maps = [{"xT": xallT[c * NSEQ:(c + 1) * NSEQ], "wflat": W, "small": SMh} for c in range(NCORES)]
    res = run_bass_kernel_spmd(nc, in_maps, core_ids=list(range(NCORES)))
    yT = np.concatenate([res.results[c]["yT"] for c in range(NCORES)], axis=0)
    y = np.ascontiguousarray(yT.transpose(0, 2, 1)).astype(np.float32)
    nb = xp.shape[0]
    return (y[:nb], y[nb:])
```

```python
import contextlib
import numpy as np
import concourse.bass as bass
import concourse.mybir as mybir
from concourse.bass_utils import run_bass_kernel_spmd

F32 = mybir.dt.float32
BF16 = mybir.dt.bfloat16
ALU = mybir.AluOpType
AF = mybir.ActivationFunctionType

D = 1024
S = 2048
DEPTH = 4
DFF = 2816
NFC = DFF // 128
NDC = D // 128
TT = 512
NT = S // TT
NCORES = 8
NSEQ = 5
CW = 31
RMS_EPS = 1e-6
LN_EPS = 1e-5

SLOT = 4096
NSLOT = 5

GU_SZ = NFC * 2 * NDC * 128
DW_SZ = NDC * NFC * 128
VG_SZ = 4 * 2 * NDC * 128
WF_SZ = NDC * 512
WO_SZ = NDC * NDC * 128
L_SZ = 2 * (GU_SZ + DW_SZ) + VG_SZ + WF_SZ + WO_SZ
OFF_GU1 = 0
OFF_D1 = OFF_GU1 + GU_SZ
OFF_VG = OFF_D1 + DW_SZ
OFF_WF = OFF_VG + VG_SZ
OFF_WO = OFF_WF + WF_SZ
OFF_GU2 = OFF_WO + WO_SZ
OFF_D2 = OFF_GU2 + GU_SZ
OFF_DFT = DEPTH * L_SZ
DFT_SZ = 4 * 16 * 2 * 512
OFF_CCS = OFF_DFT + DFT_SZ
CCS_SZ = 256
WTOT = OFF_CCS + CCS_SZ
PCH = 4096
NPCH = (WTOT + PCH - 1) // PCH
WTOTP = NPCH * PCH

SM_L = 24 + 4 * CW + 12
SM_G1, SM_GM, SM_G2, SM_CW, SM_CB, SM_LG, SM_LB = 0, 8, 16, 24, 24 + 124, 24 + 128, 24 + 132
SM_FIN = DEPTH * SM_L
SM_ID = SM_FIN + 8
SM_TOT = SM_ID + 128
NPE = 12
OFF_DIAG = WTOTP
DIAG_H = 2 * NPE * 128
DIAG_SZ = 2 * DIAG_H


class Buf:
    __slots__ = ("name", "w", "r", "ov")

    def __init__(self, name):
        self.name = name
        self.w = None
        self.r = {}
        self.ov = []


def overlap(a_list, b_list):
    for a in a_list:
        for b in b_list:
            a.ov.append(b)
            b.ov.append(a)


class Sched:
    def __init__(self, nc, stack):
        self.nc = nc
        self.stack = stack
        self.eng = {"pe": nc.tensor, "act": nc.scalar, "dve": nc.vector, "pool": nc.gpsimd, "sp": nc.sync}
        self.sems = {}
        self.cnt = {}
        self.known = {e: {} for e in self.eng}
        for e in ("pe", "act", "dve", "pool"):
            self.sem(e)

    def sem(self, key):
        if key not in self.sems:
            name = "s_" + "_".join(str(k) for k in (key if isinstance(key, tuple) else (key,)))
            self.sems[key] = self.stack.enter_context(self.nc.semaphore(name))
            self.cnt[key] = 0
        return self.sems[key]

    def need(self, e, tk, war=False):
        if tk is None:
            return
        key, val = tk
        if key == e:
            if e == "pe" or war:
                return
        if self.known[e].get(key, 0) >= val:
            return
        self.eng[e].wait_ge(self.sems[key], val)
        self.known[e][key] = val

    def deps(self, e, reads, writes):
        for b in reads:
            self.need(e, b.w)
        for b in writes:
            for bb in [b] + b.ov:
                self.need(e, bb.w)
                for k, v in bb.r.items():
                    self.need(e, (k, v), war=True)

    def commit(self, tk, reads, writes):
        for b in reads:
            b.r[tk[0]] = tk[1]
        for b in writes:
            b.w = tk
            b.r = {}

    def op(self, e, fn, reads=(), writes=()):
        self.deps(e, reads, writes)
        ins = fn(self.eng[e])
        self.cnt[e] += 1
        ins.then_inc(self.sems[e], 1)
        self.commit((e, self.cnt[e]), reads, writes)

    def mm(self, out_buf, out_ap, items):
        tk = ("pe", self.cnt["pe"] + 1)
        self.deps("pe", (), (out_buf,))
        n = len(items)
        allreads = []
        for i, (l, r, rd) in enumerate(items):
            self.deps("pe", rd, ())
            ins = self.nc.tensor.matmul(out_ap, l, r, start=(i == 0), stop=(i == n - 1))
            allreads.extend(rd)
        ins.then_inc(self.sems["pe"], 1)
        self.cnt["pe"] += 1
        self.commit(tk, allreads, (out_buf,))

    def dma(self, semkey, out_ap, in_ap, reads=(), writes=(), e="sp"):
        self.sem(semkey)
        self.deps(e, reads, writes)
        self.eng[e].dma_start(out=out_ap, in_=in_ap).then_inc(self.sems[semkey], 16)
        self.cnt[semkey] += 16
        self.commit((semkey, self.cnt[semkey]), reads, writes)


def build(nseq=NSEQ, depth=DEPTH):
    nc = bass.Bass("TRN2", target_bir_lowering=False)
    xT = nc.dram_tensor("xT", [nseq, D, S], F32, kind="ExternalInput").ap()
    wflat = nc.dram_tensor("wflat", [128, WTOTP], F32, kind="ExternalInput").ap()
    small = nc.dram_tensor("small", [128, SM_TOT], F32, kind="ExternalInput").ap()
    yT = nc.dram_tensor("yT", [nseq, D, S], F32, kind="ExternalOutput").ap()
    wbf = nc.dram_tensor("wbf", [128, WTOTP + DEPTH * DIAG_SZ], BF16, kind="Internal").ap()

    with contextlib.ExitStack() as stack:
        sc = Sched(nc, stack)
        SM = stack.enter_context(nc.sbuf_tensor("SM", [128, SM_TOT], F32))
        ONES = stack.enter_context(nc.sbuf_tensor("ONES", [128, 256], BF16))
        CCS = stack.enter_context(nc.sbuf_tensor("CCS", [128, 256], BF16))
        EPS = stack.enter_context(nc.sbuf_tensor("EPS", [128, 2], F32))
        bSM, bONES, bCCS, bNEGH = Buf("SM"), Buf("ONES"), Buf("CCS"), Buf("NEGH")

        LCH = L_SZ // PCH
        GCH = GU_SZ // PCH
        groups = [("w0a", range(0, GCH)), ("w0", range(GCH, LCH)), ("wc", range(DEPTH * LCH, NPCH))] + \
                 [(f"w{l}", range(l * LCH, (l + 1) * LCH)) for l in range(1, DEPTH)]
        bW = {}
        for gname, chunks in groups:
            bW[gname] = Buf(gname)
            for i in chunks:
                sc.sem(("cv", gname))
                nc.gpsimd.dma_start(out=wbf[:, i * PCH:(i + 1) * PCH], in_=wflat[:, i * PCH:(i + 1) * PCH],
                                    max_dma_last_dim=8192).then_inc(sc.sems[("cv", gname)], 16)
                sc.cnt[("cv", gname)] += 16
            bW[gname].w = (("cv", gname), sc.cnt[("cv", gname)])

        def wgroup(off):
            if off >= OFF_DIAG:
                return bW["dg"]
            if off >= OFF_DFT:
                return bW["wc"]
            if off < GU_SZ:
                return bW["w0a"]
            return bW[f"w{off // L_SZ}"]

        sc.dma("sm", SM[:], small[:, :], writes=(bSM,))
        sc.dma("ccs", CCS[:], wbf[:, OFF_CCS:OFF_CCS + CCS_SZ], reads=(bW["wc"],), writes=(bCCS,))

        A_X = 0
        A_HT = A_X + NDC * S * 2
        A_RING = A_HT + 2 * NDC * TT
        A_SIL = A_RING + NSLOT * SLOT
        A_RSTD = A_SIL + 3 * TT * 2
        A_C = A_RSTD + 2 * TT * 2
        CWID = S + 32
        A_MT = A_C + 4 * CWID
        A_PQ = A_MT + 2 * NDC * TT
        A_SQ = A_PQ + 4 * TT
        A_U = A_SQ + NDC * TT
        A_LN = A_U + 16 * 512
        A_CA1 = A_LN + 4 * TT * 2
        A_END_M = A_CA1 + 2 * TT * 2
        A_GT = A_U
        A_END = max(A_END_M, A_GT + NFC * TT)
        AR = stack.enter_context(nc.sbuf_tensor("ARENA", [128, A_END], BF16))

        def f32v(off, n):
            return AR[:, off:off + 2 * n].bitcast(F32)

        X = f32v(A_X, NDC * S).rearrange("p (c t) -> p c t", c=NDC)
        HT = [AR[:, A_HT + s * NDC * TT: A_HT + (s + 1) * NDC * TT].rearrange("p (c t) -> p c t", c=NDC) for s in range(2)]
        RING = [AR[:, A_RING + s * SLOT: A_RING + (s + 1) * SLOT] for s in range(NSLOT)]
        SIL = [f32v(A_SIL + s * TT * 2, TT) for s in range(3)]
        LNS = [AR[:, A_SIL + j * TT: A_SIL + (j + 1) * TT] for j in range(4)]
        RSTD = [f32v(A_RSTD + s * TT * 2, TT) for s in range(2)]
        C = AR[:, A_C:A_C + 4 * CWID].rearrange("p (j t) -> p j t", j=4)
        MT = [AR[:, A_MT + s * NDC * TT: A_MT + (s + 1) * NDC * TT].rearrange("p (c t) -> p c t", c=NDC) for s in range(2)]
        PQ = [AR[:, A_PQ + s * TT: A_PQ + (s + 1) * TT] for s in range(4)]
        SQ = AR[:, A_SQ:A_SQ + NDC * TT].rearrange("p (c t) -> p c t", c=NDC)
        CACC0 = [f32v(A_SQ + s * TT * 2, TT) for s in range(4)]
        U = AR[:, A_U:A_U + 16 * 512].rearrange("p (k c) -> p k c", k=16)
        LN = [f32v(A_LN + s * TT * 2, TT) for s in range(4)]
        CACC1 = [LN[2], LN[3], f32v(A_CA1, TT), f32v(A_CA1 + TT * 2, TT)]
        CACCS = [CACC0, CACC1]
        GT = AR[:, A_GT:A_GT + NFC * TT].rearrange("p (f t) -> p f t", f=NFC)

        bX = [[Buf(f"X{c}_{t}") for t in range(NT)] for c in range(NDC)]
        bHT = [[Buf(f"HT{s}_{c}") for c in range(NDC)] for s in range(2)]
        bRING = [Buf(f"RING{s}") for s in range(NSLOT)]
        bSIL = [Buf(f"SIL{s}") for s in range(3)]
        bLNS = [Buf(f"LNS{j}") for j in range(4)]
        for j in range(4):
            overlap([bLNS[j]], [bSIL[j // 2]])
        bRSTD = [Buf(f"RSTD{s}") for s in range(2)]
        bC = [[Buf(f"C{j}_{t}") for t in range(NT)] for j in range(4)]
        bMT = [[Buf(f"MT{s}_{c}") for c in range(NDC)] for s in range(2)]
        bPQ = [Buf(f"PQ{s}") for s in range(4)]
        bSQ = [Buf(f"SQ{c}") for c in range(NDC)]
        bCACC0 = [Buf(f"CACC{s}") for s in range(4)]
        bU = [Buf(f"U{k}") for k in range(16)]
        bLN = [Buf(f"LN{s}") for s in range(4)]
        bCACCS = [bCACC0, [bLN[2], bLN[3], Buf("CA1_2"), Buf("CA1_3")]]
        bGT = [Buf(f"GT{f}") for f in range(NFC)]
        for s in range(4):
            overlap([bCACC0[s]], [bSQ[2 * s], bSQ[2 * s + 1]])
        for f in range(NFC):
            lo, hi = f * TT, (f + 1) * TT
            for k in range(16):
                if lo < (k + 1) * 512 and k * 512 < hi:
                    overlap([bGT[f]], [bU[k]])
            for s in range(4):
                a0 = 16 * 512 + s * TT * 2
                if lo < a0 + TT * 2 and a0 < hi:
                    overlap([bGT[f]], [bLN[s]])

        PS = [stack.enter_context(nc.psum_tensor(f"PS{i}", [128, 512], F32)) for i in range(8)]
        bPS = [Buf(f"PS{i}") for i in range(8)]
        pools = {"A": [0, 1, 7], "U": [2, 3, 5], "Y": [4, 5], "S": [6], "T": [7, 4]}
        pcnt = {k: 0 for k in pools}

        def ps(pool):
            i = pools[pool][pcnt[pool] % len(pools[pool])]
            pcnt[pool] += 1
            return bPS[i], PS[i]

        sc.op("dve", lambda e: e.memset(ONES[:, 0:128], 1.0 / D), writes=(bONES,))
        sc.op("dve", lambda e: e.memset(ONES[:, 128:256], 1.0 / 512), writes=(bONES,))
        sc.op("dve", lambda e: e.memset(EPS[:, 0:1], RMS_EPS), writes=(bNEGH,))
        sc.op("dve", lambda e: e.memset(EPS[:, 1:2], LN_EPS), writes=(bNEGH,))
        sc.op("dve", lambda e: e.memset(C[:, :, 0:16], 0.0), writes=[bC[j][0] for j in range(4)])
        sc.op("dve", lambda e: e.memset(C[:, :, 15 + S:CWID], 0.0), writes=[bC[j][NT - 1] for j in range(4)])
        for l in range(depth):
            o = l * SM_L + SM_CW
            sc.op("dve", lambda e, o=o: e.tensor_scalar(SM[:, o:o + 4 * CW], SM[:, o:o + 4 * CW], 0.5, None, ALU.mult),
                  reads=(bSM,), writes=(bSM,))

        bW["dg"] = Buf("dg")
        for l in range(depth):
            o = l * SM_L + SM_CW
            for jp in range(2):
                sl = (2 * l + jp) % NSLOT
                for jj in range(2):
                    j = jp * 2 + jj
                    for k in range(NPE):
                        sc.op("dve", lambda e, j=j, jj=jj, k=k: e.tensor_scalar(
                            RING[sl][:, (jj * NPE + k) * 128:(jj * NPE + k + 1) * 128], SM[:, SM_ID:SM_ID + 128],
                            SM[:, o + j * CW + k:o + j * CW + k + 1], None, ALU.mult),
                            reads=(bSM,), writes=(bRING[sl],))
                off = OFF_DIAG + l * DIAG_SZ + jp * DIAG_H
                sc.dma("dg", wbf[:, off:off + DIAG_H], RING[sl][:, 0:DIAG_H], reads=(bRING[sl],), writes=(bW["dg"],))

        ring_i = [0]

        def ring_load(off, n):
            s = ring_i[0] % NSLOT
            ring_i[0] += 1
            sc.dma(("ring", s), RING[s][:, 0:n], wbf[:, off:off + n], reads=(wgroup(off),), writes=(bRING[s],))
            return s

        sil_i = [0]
        rstd_i = [0]

        def norm_sq(tt):
            tsl = slice(tt * TT, (tt + 1) * TT)
            for c in range(NDC):
                sc.op("act", lambda e, c=c: e.activation(SQ[:, c, :], X[:, c, tsl], AF.Square),
                      reads=(bX[c][tt],), writes=(bSQ[c],))
            for a, b in ((0, 1), (2, 3), (4, 5), (6, 7), (0, 2), (4, 6), (0, 4)):
                sc.op("dve", lambda e, a=a, b=b: e.tensor_tensor(SQ[:, a, :], SQ[:, a, :], SQ[:, b, :], ALU.add),
                      reads=(bSQ[a], bSQ[b]), writes=(bSQ[a],))

        def norm_rstd():
            bS, pS = ps("S")
            sc.mm(bS, pS[:], [(ONES[:, 0:128], SQ[:, 0, :], (bSQ[0], bONES))])
            r = rstd_i[0] % 2
            rstd_i[0] += 1
            sc.op("act", lambda e: e.activation(RSTD[r][:], pS[:], AF.Ln, bias=EPS[:, 0:1]),
                  reads=(bS, bNEGH), writes=(bRSTD[r],))
            sc.op("act", lambda e: e.activation(RSTD[r][:], RSTD[r][:], AF.Exp, scale=-0.5),
                  reads=(bRSTD[r],), writes=(bRSTD[r],))
            return r

        def norm_stats(tt):
            norm_sq(tt)
            return norm_rstd()

        def norm_h_fin(tt, goff, hs):
            r = norm_rstd()
            tsl = slice(tt * TT, (tt + 1) * TT)
            for c in range(NDC):
                sc.op("dve", lambda e, c=c: e.scalar_tensor_tensor(
                    HT[hs][:, c, :], X[:, c, tsl], SM[:, goff + c:goff + c + 1], RSTD[r][:], ALU.mult, ALU.mult),
                    reads=(bX[c][tt], bRSTD[r], bSM), writes=(bHT[hs][c],))

        def norm_h(tt, goff, hs):
            norm_sq(tt)
            norm_h_fin(tt, goff, hs)

        def ffn_phase1(l, which, tt, hs):
            base = l * L_SZ + (OFF_GU1 if which == 0 else OFF_GU2)
            slot = None
            for fc in range(NFC):
                if fc % 2 == 0:
                    slot = ring_load(base + fc * 2048, 4096)
                o = (fc % 2) * 2048
                bA, pA = ps("A")
                bU_, pU = ps("U")
                sc.mm(bA, pA[:], [(RING[slot][:, o + c * 128:o + (c + 1) * 128], HT[hs][:, c, :],
                                   (bRING[slot], bHT[hs][c])) for c in range(NDC)])
                sc.mm(bU_, pU[:], [(RING[slot][:, o + 1024 + c * 128:o + 1024 + (c + 1) * 128], HT[hs][:, c, :],
                                    (bRING[slot], bHT[hs][c])) for c in range(NDC)])
                si = sil_i[0] % 3
                sil_i[0] += 1
                sc.op("act", lambda e: e.activation(SIL[si][:], pA[:], AF.Silu), reads=(bA,), writes=(bSIL[si],))
                sc.op("dve", lambda e: e.tensor_tensor(GT[:, fc, :], pU[:], SIL[si][:], ALU.mult),
                      reads=(bU_, bSIL[si]), writes=(bGT[fc],))

        def ffn_phase2(l, which, tt, mid=None):
            base = l * L_SZ + (OFF_D1 if which == 0 else OFF_D2)
            tsl = slice(tt * TT, (tt + 1) * TT)
            for dco in range(NDC):
                if dco == 5 and mid is not None:
                    mid()
                slot = ring_load(base + dco * NFC * 128, NFC * 128)
                bY, pY = ps("Y")
                sc.mm(bY, pY[:], [(RING[slot][:, fc * 128:(fc + 1) * 128], GT[:, fc, :], (bRING[slot], bGT[fc]))
                                  for fc in range(NFC)])
                sc.op("dve", lambda e: e.scalar_tensor_tensor(X[:, dco, tsl], pY[:], 0.5, X[:, dco, tsl], ALU.mult, ALU.add),
                      reads=(bY, bX[dco][tt]), writes=(bX[dco][tt],))

        def mixer_in(l, tt, hs, npump=0, mid=None):
            base = l * L_SZ
            col0 = 15 + tt * TT
            for jp in range(2):
                slot = ring_load(base + OFF_VG + jp * 4096, 4096)
                for jj in range(2):
                    j = jp * 2 + jj
                    o = jj * 2048
                    bA, pA = ps("A")
                    bG, pG = ps("U")
                    sc.mm(bA, pA[:], [(RING[slot][:, o + c * 128:o + (c + 1) * 128], HT[hs][:, c, :],
                                       (bRING[slot], bHT[hs][c])) for c in range(NDC)])
                    sc.mm(bG, pG[:], [(RING[slot][:, o + 1024 + c * 128:o + 1024 + (c + 1) * 128], HT[hs][:, c, :],
                                       (bRING[slot], bHT[hs][c])) for c in range(NDC)])
                    si = sil_i[0] % 3
                    sil_i[0] += 1
                    sc.op("act", lambda e: e.activation(SIL[si][:], pG[:], AF.Tanh, scale=0.5), reads=(bG,), writes=(bSIL[si],))
                    sc.op("dve", lambda e: e.scalar_tensor_tensor(C[:, j, col0:col0 + TT], SIL[si][:], 1.0, pA[:], ALU.add, ALU.mult),
                          reads=(bA, bSIL[si]), writes=(bC[j][tt],))
                    pump(npump)
            if mid is not None:
                mid()
            slot = ring_load(base + OFF_WF, 4096)
            for sub in range(4):
                k = tt * 4 + sub
                bT, pT = ps("T")
                sc.mm(bT, pT[:], [(HT[hs][:, c, sub * 128:(sub + 1) * 128], RING[slot][:, c * 512:(c + 1) * 512],
                                   (bRING[slot], bHT[hs][c])) for c in range(NDC)])
                sc.op("act", lambda e: e.copy(U[:, k, :], pT[:]), reads=(bT,), writes=(bU[k],))
                pump(npump // 3)

        bgq = []

        def pump(n):
            k = 0
            while bgq and k < n:
                bgq.pop(0)[1]()
                k += 1

        def flush_through(tag):
            while bgq and bgq[0][0] <= tag:
                bgq.pop(0)[1]()

        def conv_enqueue(l, tt, tag):
            o = l * SM_L
            CACC, bCACC = CACCS[tt % 2], bCACCS[tt % 2]
            for j in range(4):
                if j % 2 == 0:
                    slot = ring_load(OFF_DIAG + l * DIAG_SZ + (j // 2) * DIAG_H, DIAG_H)
                jj = j % 2
                rd = [bRING[slot], bC[j][tt]]
                if tt > 0:
                    rd.append(bC[j][tt - 1])
                if tt < NT - 1:
                    rd.append(bC[j][tt + 1])
                bP, pP = ps("A") if j % 2 == 0 else ps("U")
                sc.mm(bP, pP[:], [(RING[slot][:, (jj * NPE + k) * 128:(jj * NPE + k + 1) * 128],
                                   C[:, j, tt * TT + k: tt * TT + k + TT], rd) for k in range(NPE)])
                b_ap = SM[:, o + SM_CB + j:o + SM_CB + j + 1]
                sc.op("dve", lambda e: e.tensor_scalar(CACC[j][:], pP[:], b_ap, None, ALU.add),
                      reads=(bP, bSM), writes=(bCACC[j],))
            for kk in range(NPE, CW):
                for j in range(4):
                    def emit(kk=kk, j=j):
                        w_ap = SM[:, o + SM_CW + j * CW + kk:o + SM_CW + j * CW + kk + 1]
                        src = C[:, j, tt * TT + kk: tt * TT + kk + TT]
                        rd = [bC[j][tt], bSM, bCACC[j]]
                        if tt > 0:
                            rd.append(bC[j][tt - 1])
                        if tt < NT - 1:
                            rd.append(bC[j][tt + 1])
                        sc.op("dve", lambda e: e.scalar_tensor_tensor(CACC[j][:], src, w_ap, CACC[j][:], ALU.mult, ALU.add),
                              reads=rd, writes=(bCACC[j],))
                    bgq.append((tag, emit))

        def ln_tile(l, tt, ms):
            o = l * SM_L
            CACC, bCACC = CACCS[tt % 2], bCACCS[tt % 2]
            bS1, pS1 = ps("S")
            items1 = []
            for j in range(4):
                sc.op("act", lambda e, j=j: e.copy(LNS[j][:], CACC[j][:]), reads=(bCACC[j],), writes=(bLNS[j],))
                items1.append((ONES[:, 128:256], LNS[j][:], (bLNS[j], bONES)))
            sc.mm(bS1, pS1[:], items1)
            bS2, pS2 = ps("T")
            items2 = []
            for j in range(4):
                sc.op("act", lambda e, j=j: e.activation(LNS[j][:], CACC[j][:], AF.Square), reads=(bCACC[j],), writes=(bLNS[j],))
                items2.append((ONES[:, 128:256], LNS[j][:], (bLNS[j], bONES)))
            sc.mm(bS2, pS2[:], items2)
            sc.op("act", lambda e: e.copy(LN[0][:], pS1[:]), reads=(bS1,), writes=(bLN[0],))
            sc.op("dve", lambda e: e.tensor_tensor(LN[1][:], LN[0][:], LN[0][:], ALU.mult), reads=(bLN[0],), writes=(bLN[1],))
            sc.op("dve", lambda e: e.tensor_tensor(LN[1][:], pS2[:], LN[1][:], ALU.subtract), reads=(bS2, bLN[1]), writes=(bLN[1],))
            sc.op("act", lambda e: e.activation(LN[1][:], LN[1][:], AF.Ln, bias=EPS[:, 1:2]), reads=(bLN[1], bNEGH), writes=(bLN[1],))
            sc.op("act", lambda e: e.activation(LN[1][:], LN[1][:], AF.Exp, scale=-0.5), reads=(bLN[1],), writes=(bLN[1],))
            for j in range(4):
                sc.op("dve", lambda e, j=j: e.tensor_tensor(CACC[j][:], CACC[j][:], LN[0][:], ALU.subtract),
                      reads=(bCACC[j], bLN[0]), writes=(bCACC[j],))
            for j in range(4):
                sc.op("dve", lambda e, j=j: e.tensor_tensor(CACC[j][:], CACC[j][:], LN[1][:], ALU.mult),
                      reads=(bCACC[j], bLN[1]), writes=(bCACC[j],))
            for j in range(4):
                sc.op("act", lambda e, j=j: e.activation(MT[ms][:, j, :], CACC[j][:], AF.Silu,
                                                         bias=SM[:, o + SM_LB + j:o + SM_LB + j + 1],
                                                         scale=SM[:, o + SM_LG + j:o + SM_LG + j + 1]),
                      reads=(bCACC[j], bSM), writes=(bMT[ms][j],))

        def dft_acc(sb, pair):
            if True:
                acc = [ps("A"), ps("A"), ps("U"), ps("U")]
                items = [[], [], [], []]
                slots = []
                for kg in range(4):
                    slot = ring_load(OFF_DFT + (sb * 4 + kg) * 4096, 4096)
                    slots.append(slot)
                    for kq in range(4):
                        k = kg * 4 + kq
                        for cc in range(2):
                            cj = pair * 2 + cc
                            for q in range(2):
                                bAcc, pAcc = acc[q * 2 + cc]
                                first = (kg == 0 and kq == 0)
                                last = (kg == 3 and kq == 3)
                                if first:
                                    sc.deps("pe", (), (bAcc,))
                                sc.deps("pe", (bRING[slot], bU[k]), ())
                                ins = nc.tensor.matmul(pAcc[:], U[:, k, cj * 128:(cj + 1) * 128],
                                                       RING[slot][:, kq * 1024 + q * 512: kq * 1024 + (q + 1) * 512],
                                                       start=first, stop=last)
                                if last:
                                    ins.then_inc(sc.sems["pe"], 1)
                                    sc.cnt["pe"] += 1
                                    tk = ("pe", sc.cnt["pe"])
                                    sc.commit(tk, [bRING[s_] for s_ in slots] + bU, (bAcc,))
                                elif kq == 3 and cc == 1 and q == 1:
                                    ins.then_inc(sc.sems["pe"], 1)
                                    sc.cnt["pe"] += 1
                                    sc.commit(("pe", sc.cnt["pe"]), [bRING[slot]], ())
                for i in (0, 2, 1, 3):
                    bAcc, pAcc = acc[i]
                    sc.op("act", lambda e, i=i, pAcc=pAcc: e.copy(PQ[i][:], pAcc[:]), reads=(bAcc,), writes=(bPQ[i],))
            return acc

        def dft_fin(sb, ms, pair, acc):
            if True:
                for cc in range(2):
                    cj = pair * 2 + cc
                    bY, pY = ps("Y")
                    sc.mm(bY, pY[:], [(CCS[:, 0:128], PQ[cc][:], (bCCS, bPQ[cc])),
                                      (CCS[:, 128:256], PQ[2 + cc][:], (bCCS, bPQ[2 + cc]))])
                    sc.op("act", lambda e, cj=cj, pY=pY: e.copy(MT[ms][:, 4 + cj, :], pY[:]), reads=(bY,), writes=(bMT[ms][4 + cj],))

        def wout_tile(l, tt, ms, npump=0):
            base = l * L_SZ + OFF_WO
            tsl = slice(tt * TT, (tt + 1) * TT)
            slot = None
            for dco in range(NDC):
                if dco % 4 == 0:
                    slot = ring_load(base + (dco // 4) * 4096, 4096)
                o = (dco % 4) * 1024
                bY, pY = ps("Y")
                sc.mm(bY, pY[:], [(RING[slot][:, o + kc * 128:o + (kc + 1) * 128], MT[ms][:, kc, :], (bRING[slot], bMT[ms][kc]))
                                  for kc in range(NDC)])
                sc.op("dve", lambda e: e.tensor_tensor(X[:, dco, tsl], pY[:], X[:, dco, tsl], ALU.add),
                      reads=(bY, bX[dco][tt]), writes=(bX[dco][tt],))
                pump(npump)

        def final_tile(i, tt):
            r = norm_stats(tt)
            tsl = slice(tt * TT, (tt + 1) * TT)
            for c in range(NDC):
                sc.op("dve", lambda e, c=c: e.scalar_tensor_tensor(
                    X[:, c, tsl], X[:, c, tsl], SM[:, SM_FIN + c:SM_FIN + c + 1], RSTD[r][:], ALU.mult, ALU.mult),
                    reads=(bX[c][tt], bRSTD[r], bSM), writes=(bX[c][tt],))
            sc.dma(("xst", tt), yT[i].rearrange("(c p) t -> p c t", p=128)[:, :, tsl], X[:, :, tsl],
                   reads=[bX[c][tt] for c in range(NDC)], e="pool")

        def load_x(i, tt, e="sp"):
            tsl = slice(tt * TT, (tt + 1) * TT)
            sc.dma(("xld", tt), X[:, :, tsl], xT[i].rearrange("(c p) t -> p c t", p=128)[:, :, tsl],
                   writes=[bX[c][tt] for c in range(NDC)], e=e)

        hs_i = [0]

        def next_hs():
            h = hs_i[0] % 2
            hs_i[0] += 1
            return h

        for i in range(nseq):
            if i == 0:
                for tt in range(NT):
                    load_x(i, tt)
                hs = next_hs()
                norm_h(0, 0 * SM_L + SM_G1, hs)
            for l in range(depth):
                o = l * SM_L
                for tt in range(NT):
                    ffn_phase1(l, 0, tt, hs)
                    hs_next = next_hs()
                    nt_, ng_ = (tt + 1, o + SM_G1) if tt + 1 < NT else (0, o + SM_GM)
                    norm_sq(nt_)
                    ffn_phase2(l, 0, tt, mid=lambda: norm_h_fin(nt_, ng_, hs_next))
                    hs = hs_next
                hsl = [hs, None, None, None]
                hsl[1] = next_hs()
                norm_sq(1)
                mixer_in(l, 0, hsl[0], mid=lambda: norm_h_fin(1, o + SM_GM, hsl[1]))
                hsl[2] = next_hs()
                norm_sq(2)
                hsl[3] = next_hs()

                def mid1():
                    norm_h_fin(2, o + SM_GM, hsl[2])
                    norm_sq(3)

                mixer_in(l, 1, hsl[1], mid=mid1)
                norm_h_fin(3, o + SM_GM, hsl[3])
                conv_enqueue(l, 0, 0)
                mixer_in(l, 2, hsl[2], npump=6)
                conv_enqueue(l, 1, 1)
                mixer_in(l, 3, hsl[3], npump=6)
                for tt in range(NT):
                    ms = tt % 2
                    acc0 = dft_acc(tt, 0)
                    flush_through(tt)
                    ln_tile(l, tt, ms)
                    if tt + 2 < NT:
                        conv_enqueue(l, tt + 2, tt + 2)
                    dft_fin(tt, ms, 0, acc0)
                    pump(27)
                    acc1 = dft_acc(tt, 1)
                    dft_fin(tt, ms, 1, acc1)
                    wout_tile(l, tt, ms, npump=2)
                    pump(33)
                assert not bgq
                hs = next_hs()
                norm_h(0, o + SM_G2, hs)
                for tt in range(NT):
                    ffn_phase1(l, 1, tt, hs)
                    if l == depth - 1 and tt >= 1:
                        final_tile(i, tt - 1)
                        if i + 1 < nseq:
                            load_x(i + 1, tt - 1, "pool")
                    hs_next = next_hs()
                    nxt = None
                    if tt + 1 < NT:
                        nxt = (tt + 1, o + SM_G2)
                    elif l + 1 < depth:
                        nxt = (0, (l + 1) * SM_L + SM_G1)
                    elif i + 1 < nseq:
                        nxt = (0, 0 * SM_L + SM_G1)
                    if nxt is not None:
                        norm_sq(nxt[0])
                        ffn_phase2(l, 1, tt, mid=lambda: norm_h_fin(nxt[0], nxt[1], hs_next))
                    else:
                        ffn_phase2(l, 1, tt)
                    hs = hs_next
            final_tile(i, NT - 1)
            if i + 1 < nseq:
                load_x(i + 1, NT - 1, "pool")
        for tt in range(NT):
            sc.need("sp", (("xst", tt), sc.cnt[("xst", tt)]))
    return nc


def _host_layout(inp):
    W = np.zeros((128, WTOTP), np.float32)

    def put(off, arr):
        a = np.ascontiguousarray(arr, dtype=np.float32).reshape(128, -1)
        W[:, off:off + a.shape[1]] = a

    for l in range(DEPTH):
        b = l * L_SZ
        for which, (kg, ku, kd, og, od) in enumerate((("ffn1_w_gate", "ffn1_w_up", "ffn1_w_down", OFF_GU1, OFF_D1),
                                                      ("ffn2_w_gate", "ffn2_w_up", "ffn2_w_down", OFF_GU2, OFF_D2))):
            g = inp[kg][l].reshape(NDC, 128, NFC, 128).transpose(1, 2, 0, 3)
            u = inp[ku][l].reshape(NDC, 128, NFC, 128).transpose(1, 2, 0, 3)
            put(b + og, np.stack([g, u], axis=2))
            d = inp[kd][l].reshape(NFC, 128, NDC, 128).transpose(1, 2, 0, 3)
            put(b + od, d)
        wi = inp["w_in"][l]
        v = wi[:, 0:512].reshape(NDC, 128, 4, 128).transpose(1, 2, 0, 3)
        gt = wi[:, 512:1024].reshape(NDC, 128, 4, 128).transpose(1, 2, 0, 3)
        put(b + OFF_VG, np.stack([v, gt], axis=2))
        f = wi[:, 1024:1536].reshape(NDC, 128, 512).transpose(1, 0, 2)
        put(b + OFF_WF, f)
        wo = inp["w_out"][l].reshape(NDC, 128, NDC, 128).transpose(1, 2, 0, 3)
        put(b + OFF_WO, wo)
    scale = 1.0 / np.sqrt(float(S * 64))
    s_idx = np.arange(S, dtype=np.int64)
    m = (s_idx[:, None] * s_idx[None, :]) % S
    ang = 2.0 * np.pi * m.astype(np.float64) / S
    cs = np.stack([np.cos(ang) * scale, -np.sin(ang) * scale], axis=0).astype(np.float32)
    t = cs.reshape(2, 16, 128, 4, 512).transpose(2, 3, 1, 0, 4)
    put(OFF_DFT, t)
    c_idx = np.arange(128)
    same = (c_idx[:, None] // 64) == (c_idx[None, :] // 64)
    a2 = 2.0 * np.pi * ((c_idx[:, None] % 64) * (c_idx[None, :] % 64) % 64) / 64.0
    cc = np.where(same, np.cos(a2), 0.0)
    sn = np.where(same, np.sin(a2), 0.0)
    put(OFF_CCS, np.concatenate([cc, sn], axis=1))

    SMh = np.zeros((128, SM_TOT), np.float32)
    for l in range(DEPTH):
        o = l * SM_L
        SMh[:, o + SM_G1:o + SM_G1 + 8] = inp["ffn1_norm"][l].reshape(8, 128).T
        SMh[:, o + SM_GM:o + SM_GM + 8] = inp["mix_norm"][l].reshape(8, 128).T
        SMh[:, o + SM_G2:o + SM_G2 + 8] = inp["ffn2_norm"][l].reshape(8, 128).T
        cw = inp["conv_w"][l].reshape(CW, 4, 128).transpose(2, 1, 0)
        SMh[:, o + SM_CW:o + SM_CW + 4 * CW] = cw.reshape(128, 4 * CW)
        SMh[:, o + SM_CB:o + SM_CB + 4] = inp["conv_b"][l].reshape(4, 128).T
        SMh[:, o + SM_LG:o + SM_LG + 4] = inp["conv_ln_g"][l].reshape(4, 128).T
        SMh[:, o + SM_LB:o + SM_LB + 4] = inp["conv_ln_b"][l].reshape(4, 128).T
    SMh[:, SM_FIN:SM_FIN + 8] = inp["final_norm"].reshape(8, 128).T
    SMh[:, SM_ID:SM_ID + 128] = np.eye(128, dtype=np.float32)
    return W, SMh


def kernel(**inputs):
    inp = {k: np.asarray(v) for k, v in inputs.items()}
    xp, xs = inp["x_prompt"], inp["x_sample"]
    xall = np.concatenate([xp, xs], axis=0)
    xallT = np.ascontiguousarray(xall.transpose(0, 2, 1))
    W, SMh = _host_layout(inp)
    nc = build()
    in_maps = [{"xT": xallT[c * NSEQ:(c + 1) * NSEQ], "wflat": W, "small": SMh} for c in range(NCORES)]
    res = run_bass_kernel_spmd(nc, in_maps, core_ids=list(range(NCORES)))
    yT = np.concatenate([res.results[c]["yT"] for c in range(NCORES)], axis=0)
    y = np.ascontiguousarray(yT.transpose(0, 2, 1)).astype(np.float32)
    nb = xp.shape[0]
    return (y[:nb], y[nb:])
```
